# Optimizing a Trainium2 kernel written in Bass

```python
import math
import jax, jax.numpy as jnp
from jax import lax
import numpy as np

D_MODEL = 1024
BATCH = 16
SEQ = 2048
DEPTH = 2

CHUNK = 64
Q_BLOCK = 128
EPS = 1e-6
MASK_VALUE = -1e30
MIN_FORGET = 1e-6

D_MIX = D_MODEL
S5_WIDTH = D_MIX // 4
DIFF_WIDTH = 3 * D_MIX // 8
HGRN_WIDTH = D_MIX - S5_WIDTH - DIFF_WIDTH

S5_GROUP = 16
S5_GROUPS = S5_WIDTH // S5_GROUP
S5_STATE = 64
S5_DT_MIN = 1e-3
S5_DT_MAX = 1e-1

DIFF_HEADS = 6
DIFF_VDIM = DIFF_WIDTH // DIFF_HEADS
DIFF_QKDIM = DIFF_VDIM // 2

HGRN_HEADS = 6
HGRN_DIM = HGRN_WIDTH // HGRN_HEADS

D_FF = 2816
CONV_WIDTH = 3

IN_SIZES = (S5_WIDTH,
            DIFF_WIDTH, DIFF_WIDTH, DIFF_WIDTH,
            HGRN_WIDTH, HGRN_WIDTH, HGRN_WIDTH, HGRN_WIDTH)
D_IN = sum(IN_SIZES)

kernel_name = "hybrid_s5_diffattn_hgrn2_convffn"


def rmsnorm(x, g):
    xf = x.astype(jnp.float32)
    r = lax.rsqrt(jnp.mean(xf * xf, axis=-1, keepdims=True) + EPS)
    return (xf * r * g.astype(jnp.float32)).astype(x.dtype)


def split_projection(proj):
    idx = [int(v) for v in np.cumsum(IN_SIZES)[:-1]]
    return jnp.split(proj, idx, axis=-1)


def s5_mixer(u, lam_re, lam_im, log_step, b_re, b_im, c_re, c_im, d, w_glu):
    bsz, s, _ = u.shape
    uf = u.astype(jnp.float32)
    ug = uf.reshape(bsz, s, S5_GROUPS, S5_GROUP)
    lam_re = lam_re.astype(jnp.float32)
    lam_im = lam_im.astype(jnp.float32)
    dt = jnp.exp(log_step.astype(jnp.float32))[:, None]
    mag = jnp.exp(lam_re * dt)
    ang = lam_im * dt
    lb_re, lb_im = mag * jnp.cos(ang), mag * jnp.sin(ang)
    den = lam_re * lam_re + lam_im * lam_im
    num_re, num_im = lb_re - 1.0, lb_im
    coef_re = (num_re * lam_re + num_im * lam_im) / den
    coef_im = (num_im * lam_re - num_re * lam_im) / den
    b_re = b_re.astype(jnp.float32)
    b_im = b_im.astype(jnp.float32)
    bb_re = coef_re[..., None] * b_re - coef_im[..., None] * b_im
    bb_im = coef_re[..., None] * b_im + coef_im[..., None] * b_re
    bu_re = jnp.einsum('gnp,bsgp->bsgn', bb_re, ug)
    bu_im = jnp.einsum('gnp,bsgp->bsgn', bb_im, ug)
    a_re = jnp.broadcast_to(lb_re, bu_re.shape)
    a_im = jnp.broadcast_to(lb_im, bu_im.shape)

    def combine(e1, e2):
        a1r, a1i, b1r, b1i = e1
        a2r, a2i, b2r, b2i = e2
        return (a2r * a1r - a2i * a1i,
                a2r * a1i + a2i * a1r,
                a2r * b1r - a2i * b1i + b2r,
                a2r * b1i + a2i * b1r + b2i)

    _, _, x_re, x_im = lax.associative_scan(combine, (a_re, a_im, bu_re, bu_im), axis=1)
    y = (jnp.einsum('gpn,bsgn->bsgp', c_re.astype(jnp.float32), x_re)
         - jnp.einsum('gpn,bsgn->bsgp', c_im.astype(jnp.float32), x_im))
    y = y.reshape(bsz, s, S5_WIDTH) + d.astype(jnp.float32) * uf
    z = jax.nn.gelu(y)
    out = z * jax.nn.sigmoid(z @ w_glu.astype(jnp.float32))
    return out.astype(u.dtype)


def diff_attention(q, k, v, lam_q1, lam_k1, lam_q2, lam_k2, subln_g, layer_idx):
    bsz, s, _ = q.shape
    q = q.reshape(bsz, s, DIFF_HEADS, 2, DIFF_QKDIM)
    k = k.reshape(bsz, s, DIFF_HEADS, 2, DIFF_QKDIM)
    v = v.reshape(bsz, s, DIFF_HEADS, DIFF_VDIM).astype(jnp.float32)
    lam_init = 0.8 - 0.6 * math.exp(-0.3 * layer_idx)
    lam = (jnp.exp(jnp.sum(lam_q1.astype(jnp.float32) * lam_k1.astype(jnp.float32)))
           - jnp.exp(jnp.sum(lam_q2.astype(jnp.float32) * lam_k2.astype(jnp.float32)))
           + lam_init)
    slopes = 2.0 ** (-8.0 * jnp.arange(1, DIFF_HEADS + 1, dtype=jnp.float32) / DIFF_HEADS)
    scale = DIFF_QKDIM ** -0.5
    k_pos = jnp.arange(s)
    n_blocks = s // Q_BLOCK
    qb = q.reshape(bsz, n_blocks, Q_BLOCK, DIFF_HEADS, 2, DIFF_QKDIM).transpose(1, 0, 2, 3, 4, 5)

    def attend(args):
        q_blk, blk = args
        q_pos = blk * Q_BLOCK + jnp.arange(Q_BLOCK)
        sc = jnp.einsum('bqhcd,bkhcd->bhcqk', q_blk, k).astype(jnp.float32) * scale
        dist = jnp.abs(q_pos[:, None] - k_pos[None, :]).astype(jnp.float32)
        visible = (k_pos[None, :] // CHUNK) <= (q_pos[:, None] // CHUNK)
        bias = jnp.where(visible[None], -slopes[:, None, None] * dist[None], MASK_VALUE)
        p = jax.nn.softmax(sc + bias[None, :, None], axis=-1)
        w = p[:, :, 0] - lam * p[:, :, 1]
        return jnp.einsum('bhqk,bkhd->bqhd', w, v)

    o = lax.map(attend, (qb, jnp.arange(n_blocks)))
    o = o.transpose(1, 0, 2, 3, 4).reshape(bsz, s, DIFF_HEADS, DIFF_VDIM)
    o = rmsnorm(o, subln_g) * (1.0 - lam_init)
    return o.reshape(bsz, s, DIFF_WIDTH).astype(q.dtype)


def hgrn2_mixer(q, f_raw, i, g, lb, norm_g):
    bsz, s, _ = q.shape
    nc = s // CHUNK
    shp = (bsz, s, HGRN_HEADS, HGRN_DIM)
    lb = lb.astype(jnp.float32).reshape(HGRN_HEADS, HGRN_DIM)
    f = lb + (1.0 - lb) * jax.nn.sigmoid(f_raw.astype(jnp.float32).reshape(shp))
    log_f = jnp.log(jnp.maximum(f, MIN_FORGET))
    kk = 1.0 - f
    qq = jax.nn.silu(q.astype(jnp.float32).reshape(shp))
    vv = i.astype(jnp.float32).reshape(shp)

    def to_chunks(t):
        return t.reshape(bsz, nc, CHUNK, HGRN_HEADS, HGRN_DIM).transpose(1, 0, 3, 2, 4)

    qc, kc, vc, lfc = to_chunks(qq), to_chunks(kk), to_chunks(vv), to_chunks(log_f)
    bc = jnp.cumsum(lfc, axis=3)
    causal = jnp.tril(jnp.ones((CHUNK, CHUNK), dtype=bool))[:, :, None]

    def step(state, xs):
        qt, kt, vt, bt = xs
        rel = bt[:, :, :, None, :] - bt[:, :, None, :, :]
        decay = jnp.where(causal, jnp.exp(jnp.where(causal, rel, 0.0)), 0.0)
        scores = jnp.einsum('bhtd,bhsd,bhtsd->bhts', qt, kt, decay)
        o = (jnp.einsum('bhts,bhsv->bhtv', scores, vt)
             + jnp.einsum('bhtd,bhdv->bhtv', qt * jnp.exp(bt), state))
        b_last = bt[:, :, -1:, :]
        state = (jnp.exp(b_last[:, :, 0, :])[..., None] * state
                 + jnp.einsum('bhsd,bhsv->bhdv', kt * jnp.exp(b_last - bt), vt))
        return state, o

    state0 = jnp.zeros((bsz, HGRN_HEADS, HGRN_DIM, HGRN_DIM), jnp.float32)
    _, o = lax.scan(step, state0, (qc, kc, vc, bc))
    o = o.transpose(1, 0, 3, 2, 4).reshape(shp)
    o = rmsnorm(o, norm_g) * jax.nn.silu(g.astype(jnp.float32).reshape(shp))
    return o.reshape(bsz, s, HGRN_WIDTH).astype(q.dtype)


def conv_glu_ffn(h, w_up, w_gate, conv_w, conv_b, w_down):
    up = h @ w_up
    gate = h @ w_gate
    up = lax.conv_general_dilated(up, conv_w[:, None, :], window_strides=(1,),
                                  padding=[(CONV_WIDTH - 1, 0)],
                                  dimension_numbers=('NWC', 'WIO', 'NWC'),
                                  feature_group_count=D_FF) + conv_b
    return (jax.nn.gelu(up) * gate) @ w_down


def setup_inputs(seed: int = 0) -> dict:
    key = jax.random.key(seed)
    ks = jax.random.split(key, 32)
    f32 = jnp.float32
    L, G, N, P = DEPTH, S5_GROUPS, S5_STATE, S5_GROUP

    def nrm(k, shape, scale):
        return jax.random.normal(k, shape, f32) * scale

    x = nrm(ks[0], (BATCH, SEQ, D_MODEL), 1.0)
    norm_mix_g = 1.0 + nrm(ks[1], (L, D_MODEL), 0.02)
    w_in = nrm(ks[2], (L, D_MODEL, D_IN), D_MODEL ** -0.5)
    s5_lambda_re = jnp.minimum(-0.5 + nrm(ks[3], (L, G, N), 0.02), -0.1)
    s5_lambda_im = (math.pi * jnp.arange(N, dtype=f32))[None, None, :] + nrm(ks[4], (L, G, N), 0.01)
    s5_log_step = jax.random.uniform(ks[5], (L, G), f32, math.log(S5_DT_MIN), math.log(S5_DT_MAX))
    s5_b_re = nrm(ks[6], (L, G, N, P), (2.0 * P) ** -0.5)
    s5_b_im = nrm(ks[7], (L, G, N, P), (2.0 * P) ** -0.5)
    s5_c_re = nrm(ks[8], (L, G, P, N), (2.0 * N) ** -0.5 * 4.0)
    s5_c_im = nrm(ks[9], (L, G, P, N), (2.0 * N) ** -0.5 * 4.0)
    s5_d = nrm(ks[10], (L, S5_WIDTH), 1.0)
    s5_w_glu = nrm(ks[11], (L, S5_WIDTH, S5_WIDTH), S5_WIDTH ** -0.5)
    diff_lambda_q1 = nrm(ks[12], (L, DIFF_QKDIM), 0.1)
    diff_lambda_k1 = nrm(ks[13], (L, DIFF_QKDIM), 0.1)
    diff_lambda_q2 = nrm(ks[14], (L, DIFF_QKDIM), 0.1)
    diff_lambda_k2 = nrm(ks[15], (L, DIFF_QKDIM), 0.1)
    diff_subln_g = 1.0 + nrm(ks[16], (L, DIFF_VDIM), 0.02)
    hgrn_lb_logits = nrm(ks[17], (L, HGRN_WIDTH), 0.5)
    hgrn_norm_g = 1.0 + nrm(ks[18], (L, HGRN_DIM), 0.02)
    w_out = nrm(ks[19], (L, D_MIX, D_MODEL), D_MIX ** -0.5)
    norm_ffn_g = 1.0 + nrm(ks[20], (L, D_MODEL), 0.02)
    w_up = nrm(ks[21], (L, D_MODEL, D_FF), D_MODEL ** -0.5)
    w_gate = nrm(ks[22], (L, D_MODEL, D_FF), D_MODEL ** -0.5)
    conv_w = nrm(ks[23], (L, CONV_WIDTH, D_FF), CONV_WIDTH ** -0.5)
    conv_b = nrm(ks[24], (L, D_FF), 0.02)
    w_down = nrm(ks[25], (L, D_FF, D_MODEL), D_FF ** -0.5)
    final_norm_g = 1.0 + nrm(ks[26], (D_MODEL,), 0.02)
    return {"x": x, "norm_mix_g": norm_mix_g, "w_in": w_in,
            "s5_lambda_re": s5_lambda_re, "s5_lambda_im": s5_lambda_im, "s5_log_step": s5_log_step,
            "s5_b_re": s5_b_re, "s5_b_im": s5_b_im, "s5_c_re": s5_c_re, "s5_c_im": s5_c_im,
            "s5_d": s5_d, "s5_w_glu": s5_w_glu,
            "diff_lambda_q1": diff_lambda_q1, "diff_lambda_k1": diff_lambda_k1,
            "diff_lambda_q2": diff_lambda_q2, "diff_lambda_k2": diff_lambda_k2,
            "diff_subln_g": diff_subln_g, "hgrn_lb_logits": hgrn_lb_logits, "hgrn_norm_g": hgrn_norm_g,
            "w_out": w_out, "norm_ffn_g": norm_ffn_g, "w_up": w_up, "w_gate": w_gate,
            "conv_w": conv_w, "conv_b": conv_b, "w_down": w_down, "final_norm_g": final_norm_g}


def reference(x, norm_mix_g, w_in, s5_lambda_re, s5_lambda_im, s5_log_step,
              s5_b_re, s5_b_im, s5_c_re, s5_c_im, s5_d, s5_w_glu,
              diff_lambda_q1, diff_lambda_k1, diff_lambda_q2, diff_lambda_k2,
              diff_subln_g, hgrn_lb_logits, hgrn_norm_g, w_out, norm_ffn_g,
              w_up, w_gate, conv_w, conv_b, w_down, final_norm_g):
    p_lb = jax.nn.softmax(hgrn_lb_logits.astype(jnp.float32), axis=0)
    lower_bounds = jnp.cumsum(p_lb, axis=0) - p_lb[0:1]
    for l in range(DEPTH):
        h = rmsnorm(x, norm_mix_g[l])
        proj = h @ w_in[l]
        u, dq, dk, dv, cq, cf, ci, cg = split_projection(proj)
        y_a = s5_mixer(u, s5_lambda_re[l], s5_lambda_im[l], s5_log_step[l],
                       s5_b_re[l], s5_b_im[l], s5_c_re[l], s5_c_im[l], s5_d[l], s5_w_glu[l])
        y_b = diff_attention(dq, dk, dv, diff_lambda_q1[l], diff_lambda_k1[l],
                             diff_lambda_q2[l], diff_lambda_k2[l], diff_subln_g[l], l)
        y_c = hgrn2_mixer(cq, cf, ci, cg, lower_bounds[l], hgrn_norm_g[l])
        mix = jnp.concatenate([y_a, y_b, y_c], axis=-1)
        x = x + (mix @ w_out[l]).astype(x.dtype)
        h = rmsnorm(x, norm_ffn_g[l])
        x = x + conv_glu_ffn(h, w_up[l], w_gate[l], conv_w[l], conv_b[l], w_down[l]).astype(x.dtype)
    return rmsnorm(x, final_norm_g)
```

```python
import os
import math
import contextlib
import numpy as np
import concourse.bass as bass
import concourse.mybir as mybir
from concourse.bass_utils import run_bass_kernel_spmd

F32 = mybir.dt.float32
BF16 = mybir.dt.bfloat16
I32 = mybir.dt.int32
AF = mybir.ActivationFunctionType
ALU = mybir.AluOpType
AX = mybir.AxisListType

ALL_ENG = ("sync", "scalar", "gpsimd", "vector", "tensor")

D = 1024
SEQ = 2048
NTT = 4
TT = 512
DEPTH = 2
D_IN = 2944
D_FF = 2816
NFF = 22
EPS = 1e-6
C_U, C_Q, C_K, C_V = 0, 256, 640, 1024
C_CQ, C_CF, C_CI, C_CG = 1408, 1792, 2176, 2560
SCALE = 32 ** -0.5
SLOPES = [2.0 ** (-8.0 * (h + 1) / 6.0) for h in range(6)]
TWO_PI = 2.0 * math.pi


class Prog:
    def __init__(self, nc):
        self.nc = nc
        self.ops = []
        self.force = {}
        self.phase = 0

    def op(self, eng, fn, reads=(), writes=(), dma=None, nophase=False, force=()):
        reads = list(reads)
        if not nophase:
            reads.append(("phase",))
        self.ops.append((eng, fn, tuple(reads), tuple(writes), dma, False))
        self.force[len(self.ops) - 1] = tuple(force)
        return len(self.ops) - 1

    def barrier(self, eng_fn):
        self.ops.append(("vector", eng_fn, (), (("phase",),), None, True))

    def emit(self, final_wait_keys=()):
        nc = self.nc
        ops = self.ops
        n = len(ops)
        cnt = {}
        ev = [None] * n
        for i, (eng, fn, r, w, dma, bar) in enumerate(ops):
            key = ("d", dma) if dma is not None else ("c", eng)
            cnt[key] = cnt.get(key, 0) + (16 if dma is not None else 1)
            ev[i] = (key, cnt[key])
        last_w = {}
        readers = {}
        deps = [None] * n
        last_of_eng = {}
        last_of_dma = {}
        for i, (eng, fn, r, w, dma, bar) in enumerate(ops):
            d = set()
            for t in r:
                j = last_w.get(t)
                if j is not None:
                    d.add(j)
            for t in w:
                j = last_w.get(t)
                if j is not None:
                    d.add(j)
                for j in readers.get(t, ()):
                    d.add(j)
            if bar:
                d.update(last_of_eng.values())
                d.update(last_of_dma.values())
            d.update(self.force.get(i, ()))
            d.discard(i)
            deps[i] = d
            for t in r:
                readers.setdefault(t, []).append(i)
            for t in w:
                last_w[t] = i
                readers[t] = []
            if dma is None:
                last_of_eng[eng] = i
            else:
                last_of_dma[dma] = i
        dma_cum = {}
        known = {e: {} for e in ALL_ENG}
        snap = [None] * n
        waits = [None] * n
        for i, (eng, fn, r, w, dma, bar) in enumerate(ops):
            need = {}
            for j in deps[i]:
                jeng, _, _, _, jdma, _ = ops[j]
                key, val = ev[j]
                if jdma is not None:
                    val = dma_cum[key]
                elif jeng == "tensor" and eng == "tensor" and dma is None and j not in self.force.get(i, ()):
                    continue
                if need.get(key, 0) < val:
                    need[key] = val
            kn = known[eng]
            wl = {}
            for key, val in need.items():
                if kn.get(key, 0) >= val:
                    continue
                wl[key] = val
            for key, val in wl.items():
                kn[key] = val
            for j in deps[i]:
                if ops[j][4] is None and snap[j] is not None:
                    key, val = ev[j]
                    if key in wl:
                        for k2, v2 in snap[j].items():
                            if kn.get(k2, 0) < v2:
                                kn[k2] = v2
            waits[i] = wl
            if dma is not None:
                dma_cum[ev[i][0]] = ev[i][1]
            else:
                snap[i] = dict(kn)
        stack = contextlib.ExitStack()
        sems = {}
        for key in cnt:
            sems[key] = stack.enter_context(nc.semaphore("s_%s_%s" % key))
        self.nsem = len(sems)
        per_eng = {e: [] for e in ALL_ENG}
        for i, o in enumerate(ops):
            per_eng[o[0]].append(i)
        fin = [(("d", k), cnt[("d", k)]) for k in final_wait_keys]

        def make(e):
            def body(engh):
                for i in per_eng[e]:
                    eng, fn, r, w, dma, bar = ops[i]
                    for key, val in waits[i].items():
                        engh.wait_ge(sems[key], val)
                    ins = fn(engh)
                    key, val = ev[i]
                    ins.then_inc(sems[key], 16 if dma is not None else 1)
                if e == "sync":
                    for key, val in fin:
                        engh.wait_ge(sems[key], val)
            return body

        with stack:
            with nc.Block() as block:
                block.sync(make("sync"))
                block.scalar(make("scalar"))
                block.gpsimd(make("gpsimd"))
                block.vector(make("vector"))
                block.tensor(make("tensor"))


def _bf16_round(a):
    a = np.asarray(a, np.float32)
    u = a.view(np.uint32).astype(np.uint64)
    u = ((u + 0x7FFF + ((u >> 16) & 1)) >> 16) << 16
    return u.astype(np.uint32).view(np.float32)


def make_consts():
    c = {}
    c["c_ident"] = np.eye(128, dtype=np.float32)
    c["c_ones"] = np.ones((128, 128), np.float32)
    bo = np.zeros((128, 128), np.float32)
    bo[:64, :64] = 1.0
    bo[64:, 64:] = 1.0
    c["c_bones"] = bo
    z = np.zeros((128, 8, 240), np.float32)
    for b in range(8):
        for q in range(16):
            z[16 * b + q, b, 112 + q] = 1.0
    c["c_zband"] = z
    wm = np.zeros((128, 128), np.float32)
    for s in range(8):
        for t in range(s, 8):
            wm[16 * s:16 * s + 16, 16 * t:16 * t + 16] = 1.0
    c["c_wmask"] = wm
    hm = np.zeros((128, 64), np.float32)
    for s in range(64):
        hm[s, s:] = 1.0
        hm[64 + s, s:] = 1.0
    c["c_hmask"] = hm
    sm = np.ones((128, 512), np.float32)
    sm[:, 0::64] = 0.0
    c["c_scanmask"] = sm
    corr = np.zeros((128, 6, 128), np.float32)
    kk = np.arange(128)[:, None]
    qq = np.arange(128)[None, :]
    for h in range(6):
        sp = SLOPES[h] / SCALE
        same = (kk // 64) == (qq // 64)
        fut = kk > qq
        corr[:, h, :] = np.where(fut & same, -2.0 * sp * (kk - qq), 0.0)
        corr[:, h, :] = np.where((kk // 64) > (qq // 64), -1e30, corr[:, h, :])
    c["c_corr"] = corr
    kv = np.array([7, 6, 5, 4, 3, 2, 1, 0] + [-1, -2, -3, -4, -5, -6, -7, -8] + [1, 2, 3, 4, 5, 6, 7, 8], np.float32)
    c["c_kvec"] = np.tile(kv[None, :], (128, 1))
    c["c_mvec"] = np.tile(np.arange(1, 257, dtype=np.float32)[None, :], (128, 1))
    ktab = np.zeros((6, 6, SEQ), np.float32)
    qtab = np.zeros((6, 6, SEQ), np.float32)
    pos = np.arange(SEQ)
    kl = (pos % 128).astype(np.float32)
    ql = pos % 512
    qa = ((ql // 256) * 256).astype(np.float32)
    qb = (ql % 256).astype(np.float32)
    for h in range(6):
        sp = np.float32(SLOPES[h] / SCALE)
        hi = _bf16_round(np.array([sp], np.float32))[0]
        lo = _bf16_round(np.array([sp - hi], np.float32))[0]
        ktab[h, 0] = kl; ktab[h, 1] = kl; ktab[h, 2] = hi; ktab[h, 3] = lo; ktab[h, 4] = hi; ktab[h, 5] = lo
        qtab[h, 0] = hi; qtab[h, 1] = lo; qtab[h, 2] = -qa; qtab[h, 3] = -qa; qtab[h, 4] = -qb; qtab[h, 5] = -qb
    c["c_ktab"] = ktab
    c["c_qtab"] = qtab
    return c


CONST_SHAPES = {"c_ident": [128, 128], "c_ones": [128, 128], "c_bones": [128, 128], "c_zband": [128, 8, 240],
                "c_wmask": [128, 128], "c_hmask": [128, 64], "c_scanmask": [128, 512], "c_corr": [128, 6, 128],
                "c_kvec": [128, 24], "c_mvec": [128, 256], "c_ktab": [6, 6, SEQ], "c_qtab": [6, 6, SEQ]}


def layout_params(inp):
    L = DEPTH
    f = lambda a: np.ascontiguousarray(a, dtype=np.float32)
    p = {}
    def feat_tiles(a, ntile):
        return f(a.reshape(a.shape[0], ntile, 128).transpose(2, 0, 1))
    p["p_gmix"] = feat_tiles(inp["norm_mix_g"], 8)
    p["p_gffn"] = feat_tiles(inp["norm_ffn_g"], 8)
    p["p_gfin"] = f(inp["final_norm_g"].reshape(8, 128).T)
    def pairs(a):
        sh = a.shape
        a = a.reshape(sh[0], 8, 2, 64, *sh[3:])
        perm = (2, 3, 0, 1) + tuple(range(4, a.ndim))
        a = a.transpose(perm)
        return f(a.reshape(128, sh[0], 8, *sh[3:]))
    p["p_lre"] = pairs(inp["s5_lambda_re"])
    p["p_lim"] = pairs(inp["s5_lambda_im"])
    p["p_lstep"] = pairs(np.repeat(inp["s5_log_step"][:, :, None], 64, axis=2))
    p["p_bre"] = pairs(inp["s5_b_re"])
    p["p_bim"] = pairs(inp["s5_b_im"])
    p["p_cre"] = pairs(inp["s5_c_re"].transpose(0, 1, 3, 2))
    p["p_cim"] = pairs(inp["s5_c_im"].transpose(0, 1, 3, 2))
    p["p_s5d"] = feat_tiles(inp["s5_d"], 2)
    for k in ("diff_lambda_q1", "diff_lambda_k1", "diff_lambda_q2", "diff_lambda_k2"):
        p["p_" + k[5:]] = f(np.tile(inp[k][None, :, :], (128, 1, 1)))
    p["p_subg"] = f(np.tile(inp["diff_subln_g"].T, (2, 1)))
    p["p_lbl"] = feat_tiles(inp["hgrn_lb_logits"], 3)
    p["p_hng"] = f(np.tile(inp["hgrn_norm_g"].T, (2, 1)))
    p["p_convw"] = f(inp["conv_w"].reshape(L, 3, NFF, 128).transpose(3, 0, 1, 2))
    p["p_convb"] = feat_tiles(inp["conv_b"], NFF)
    return p


PARAM_SHAPES = {"p_gmix": [128, 2, 8], "p_gffn": [128, 2, 8], "p_gfin": [128, 8],
                "p_lre": [128, 2, 8], "p_lim": [128, 2, 8], "p_lstep": [128, 2, 8],
                "p_bre": [128, 2, 8, 16], "p_bim": [128, 2, 8, 16], "p_cre": [128, 2, 8, 16], "p_cim": [128, 2, 8, 16],
                "p_s5d": [128, 2, 2], "p_lambda_q1": [128, 2, 32], "p_lambda_k1": [128, 2, 32],
                "p_lambda_q2": [128, 2, 32], "p_lambda_k2": [128, 2, 32], "p_subg": [128, 2],
                "p_lbl": [128, 2, 3], "p_hng": [128, 2], "p_convw": [128, 2, 3, NFF], "p_convb": [128, 2, NFF]}

BIGW = {"w_in": [DEPTH, D, D_IN], "w_out": [DEPTH, D, D], "w_up": [DEPTH, D, D_FF], "w_gate": [DEPTH, D, D_FF],
        "w_down": [DEPTH, D_FF, D], "s5_w_glu": [DEPTH, 256, 256]}

ARENA_W = 15104


class Builder:
    def __init__(self, nseq=2, depth=DEPTH, taps=(), phases=("s5", "att", "hgrn", "ffn")):
        self.nseq, self.depth, self.taps, self.phases = nseq, depth, set(taps), set(phases)
        nc = self.nc = bass.Bass("TRN2", target_bir_lowering=False)
        self.P = Prog(nc)
        self.st = contextlib.ExitStack()
        self.fin_keys = []
        self.dram = {}
        for k, sh in list(CONST_SHAPES.items()) + list(PARAM_SHAPES.items()) + list(BIGW.items()):
            self.dram[k] = nc.dram_tensor(k, sh, F32, kind="ExternalInput").ap()
        self.x_in = nc.dram_tensor("xT", [nseq, D, SEQ], F32, kind="ExternalInput").ap()
        self.y_out = nc.dram_tensor("yT", [nseq, D, SEQ], F32, kind="ExternalOutput").ap()
        self.tap_out = {}
        self.bankrr = 0
        self.pe_last = {}
        self.rr = list(range(7))
        self.uid = 0
        self.alloc()

    def sb(self, name, shape, dt):
        return self.st.enter_context(self.nc.sbuf_tensor(name, shape, dt))

    def alloc(self):
        nc = self.nc
        self.xT = self.sb("xT_sb", [128, 8, SEQ], F32)
        self.hT = self.sb("hT_sb", [128, 8, SEQ + 2], BF16)
        self.ringS = [self.sb("ringS%d" % i, [128, 1024], BF16) for i in range(6)]
        self.ringB = [self.sb("ringB%d" % i, [128, NFF * 128], BF16) for i in range(2)]
        self.rs_i = 0
        self.rb_i = 0
        self.c_ident = self.sb("k_ident", [128, 128], BF16)
        self.c_identf = self.sb("k_identf", [128, 128], F32)
        self.c_ones = self.sb("k_ones", [128, 128], BF16)
        self.c_bones = self.sb("k_bones", [128, 128], BF16)
        self.c_zband = self.sb("k_zband", [128, 8, 240], BF16)
        self.c_wmask = self.sb("k_wmask", [128, 128], F32)
        self.c_hmask = self.sb("k_hmask", [128, 64], F32)
        self.c_scanmask = self.sb("k_scanmask", [128, 512], BF16)
        self.c_corr = self.sb("k_corr", [128, 6, 128], F32)
        self.c_kvec = self.sb("k_kvec", [128, 24], F32)
        self.c_mvec = self.sb("k_mvec", [128, 256], F32)
        self.prm = {}
        for k, sh in PARAM_SHAPES.items():
            self.prm[k] = self.sb("sb_" + k, sh, F32)
        self.lamv = self.sb("lamv", [128, 2], F32)
        self.lb = self.sb("lb", [128, 2, 3], F32)
        self.oml = self.sb("oml", [128, 2, 3], F32)
        self.s5W = self.sb("s5W", [128, 16, 128], BF16)
        self.s5W2 = self.sb("s5W2", [128, 2, 8, 128], BF16)
        self.s5M = self.sb("s5M", [128, 2, 8, 128], BF16)
        self.s5rp = self.sb("s5rp", [128, 2, 8], F32)
        self.arena = self.sb("arena", [128, ARENA_W], F32)
        self.ps = self.st.enter_context(nc.psum_tensor("ps_main", [128, 7 * 512], F32))
        self.psb = self.st.enter_context(nc.psum_tensor("ps_bf", [128, 1024], BF16))
        self.scr_W = nc.dram_tensor("scr_W", [DEPTH, 128, 16 * 128], BF16, kind="Internal").ap()
        self.scr_W2 = nc.dram_tensor("scr_W2", [DEPTH, 128, 2 * 8 * 128], BF16, kind="Internal").ap()
        self.scr_M = nc.dram_tensor("scr_M", [DEPTH, 128, 2 * 8 * 128], BF16, kind="Internal").ap()
        self.scr_rp = nc.dram_tensor("scr_rp", [DEPTH, 128, 16], F32, kind="Internal").ap()

    def av(self, off, words, dt=F32):
        v = self.arena[:, off:off + words]
        if dt == BF16:
            v = v.bitcast(BF16)
        elif dt == I32:
            v = v.bitcast(I32)
        return v

    def bank(self):
        rr = self.rr
        self.bankrr = (self.bankrr + 1) % len(rr)
        return rr[self.bankrr]

    def pb(self, b, n=512, off=0):
        return self.ps[:, b * 512 + off: b * 512 + off + n]

    def mm(self, out, lhsT, rhs, start, stop, r, w, sgc=False):
        rt = (lhsT.base_partition(), lhsT.partition_size())
        for t in w:
            prev = self.pe_last.get(t)
            if prev is not None and prev != rt:
                self.P.op("sync", lambda e: e.nop(nofuse=True), reads=[t], writes=[t])
        kw = {"skip_group_check": True} if sgc else {}
        self.P.op("tensor", lambda e: e.matmul(out, lhsT, rhs, start=start, stop=stop, **kw), reads=r, writes=w)
        for t in w:
            self.pe_last[t] = rt

    def tr(self, out, in_, ident, r, w):
        self.P.op("tensor", lambda e: e.transpose(out, in_, ident), reads=r, writes=w)

    def act(self, out, in_, func, r, w, bias=None, scale=None):
        kw = {}
        if bias is not None:
            kw["bias"] = bias
        if scale is not None:
            kw["scale"] = scale
        self.P.op("scalar", lambda e: e.activation(out=out, in_=in_, func=func, **kw), reads=r, writes=w)

    def tt(self, out, a, b, op, r, w, eng="vector"):
        self.P.op(eng, lambda e: e.tensor_tensor(out, a, b, op=op), reads=r, writes=w)

    def ts(self, out, a, s1, s2, op0, op1, r, w, eng="vector"):
        if s2 is None:
            self.P.op(eng, lambda e: e.tensor_scalar(out, a, s1, None, op0=op0), reads=r, writes=w)
        else:
            self.P.op(eng, lambda e: e.tensor_scalar(out, a, s1, s2, op0=op0, op1=op1), reads=r, writes=w)

    def stt(self, out, a, s, b, op0, op1, r, w, eng="vector"):
        self.P.op(eng, lambda e: e.scalar_tensor_tensor(out, a, s, b, op0=op0, op1=op1), reads=r, writes=w)

    def cp(self, out, in_, r, w, eng="vector"):
        self.P.op(eng, lambda e: e.tensor_copy(out, in_), reads=r, writes=w)

    def recip(self, out, in_, r, w):
        self.P.op("vector", lambda e: e.reciprocal(out, in_), reads=r, writes=w)

    def memset(self, out, val, r, w, eng="vector"):
        self.P.op(eng, lambda e: e.memset(out, val), reads=r, writes=w)

    def scan(self, out, d0, d1, r, w):
        self.P.op("vector", lambda e: e.tensor_tensor_scan(out, d0, d1, 0.0, op0=ALU.mult, op1=ALU.add), reads=r, writes=w)

    def dma(self, eng, out, in_, r, w, key, nophase=False):
        self.P.op(eng, lambda e: e.dma_start(out=out, in_=in_), reads=r, writes=w, dma=key, nophase=nophase)

    def barrier(self):
        t = self.bar_tile
        self.P.barrier(lambda e: e.memset(t[:], 0.0))

    def tap(self, name, ap, shape, r):
        if name not in self.taps:
            return
        self.uid += 1
        o = self.nc.dram_tensor("tap_" + name, shape, ap.dtype, kind="ExternalOutput").ap()
        key = "tap%d" % self.uid
        self.dma("sync", o, ap, r, [], key)
        self.fin_keys.append(key)

    def load_cols(self, wl, c0, ncols):
        i = self.rs_i
        self.rs_i = (self.rs_i + 1) % len(self.ringS)
        slot = self.ringS[i]
        dst = slot[:, 0:8 * ncols].rearrange("p (k c) -> p k c", c=ncols)
        src = wl[:, c0:c0 + ncols].rearrange("(k p) c -> p k c", p=128)
        self.dma("gpsimd", dst, src, [], [("rs", i)], "rs%d" % i, nophase=True)
        return dst, ("rs", i)

    def load_rows(self, wl, r0, ncols):
        i = self.rs_i
        self.rs_i = (self.rs_i + 1) % len(self.ringS)
        slot = self.ringS[i]
        dst = slot[:, 0:ncols]
        self.dma("gpsimd", dst, wl[r0:r0 + 128, 0:ncols], [], [("rs", i)], "rs%d" % i, nophase=True)
        return dst, ("rs", i)

    def load_bigcols(self, wl, c0, nk):
        i = self.rb_i
        self.rb_i = (self.rb_i + 1) % len(self.ringB)
        slot = self.ringB[i]
        dst = slot[:, 0:nk * 128].rearrange("p (k c) -> p k c", c=128)
        src = wl[:, c0:c0 + 128].rearrange("(k p) c -> p k c", p=128)
        self.dma("gpsimd", dst, src, [], [("rb", i)], "rb%d" % i, nophase=True)
        return dst, ("rb", i)

    def setup(self):
        d = self.dram
        self.bar_tile = self.sb("bar_tile", [128, 2], F32)
        ld = 0
        def ldc(dst, src, cast):
            nonlocal ld
            ld += 1
            self.dma("gpsimd" if cast else "sync", dst, src, [], [dst.name if hasattr(dst, "name") else ld], "ldc%d" % ld)
        ldc(self.c_ident[:], d["c_ident"], True)
        ldc(self.c_identf[:], d["c_ident"], False)
        ldc(self.c_ones[:], d["c_ones"], True)
        ldc(self.c_bones[:], d["c_bones"], True)
        ldc(self.c_zband[:], d["c_zband"], True)
        ldc(self.c_wmask[:], d["c_wmask"], False)
        ldc(self.c_hmask[:], d["c_hmask"], False)
        ldc(self.c_scanmask[:], d["c_scanmask"], True)
        ldc(self.c_corr[:], d["c_corr"], False)
        ldc(self.c_kvec[:], d["c_kvec"], False)
        ldc(self.c_mvec[:], d["c_mvec"], False)
        for k in PARAM_SHAPES:
            ldc(self.prm[k][:], d[k], False)
        self.memset(self.hT[:, :, 0:2], 0.0, [], ["hTpad"])
        self.barrier()
        self.derive_small()
        self.barrier()
        for l in range(self.depth):
            if "s5" in self.phases:
                self.s5_prep(l)
                self.barrier()

    def derive_small(self):
        A = self.av
        tmp = A(0, 32)
        acc = A(32, 8)
        for l in range(2):
            for i, (a, b) in enumerate((("p_lambda_q1", "p_lambda_k1"), ("p_lambda_q2", "p_lambda_k2"))):
                self.tt(tmp, self.prm[a][:, l, :], self.prm[b][:, l, :], ALU.mult, [], ["ds_tmp"])
                col = acc[:, 2 * l + i: 2 * l + i + 1]
                self.P.op("vector", lambda e, col=col: e.reduce_sum(col, tmp, axis=AX.X), reads=["ds_tmp"], writes=["ds_acc"])
        ex = A(40, 8)
        self.act(ex[:, 0:4], acc[:, 0:4], AF.Exp, ["ds_acc"], ["ds_ex"])
        for l in range(2):
            lam_init = 0.8 - 0.6 * math.exp(-0.3 * l)
            self.stt(self.lamv[:, l:l + 1], ex[:, 2 * l + 1:2 * l + 2], -lam_init, ex[:, 2 * l:2 * l + 1],
                     ALU.add, ALU.subtract, ["ds_ex"], ["lamv"])
        dl = A(48, 3)
        self.tt(dl, self.prm["p_lbl"][:, 1, :], self.prm["p_lbl"][:, 0, :], ALU.subtract, [], ["ds_dl"])
        self.memset(self.lb[:, 0, :], 0.0, [], ["lb"])
        self.act(self.lb[:, 1, :], dl, AF.Sigmoid, ["ds_dl"], ["lb"])
        self.ts(self.oml[:], self.lb[:], -1.0, 1.0, ALU.mult, ALU.add, ["lb"], ["oml"])

    def rmsnorm(self, gsb, final_out=None):
        A = self.av
        sq = [A(0, 256, BF16), A(256, 256, BF16)]
        rt = A(512, 512)
        rstd = A(1024, 512)
        stage = [A(1536, 512), A(2048, 512)]
        for tt in range(NTT):
            cs = slice(tt * TT, (tt + 1) * TT)
            b = self.bank()
            for k in range(8):
                s = sq[k % 2]
                self.act(s, self.xT[:, k, cs], AF.Square, [("xT", k, tt)], [("n_sq", k % 2)])
                self.mm(self.pb(b), self.c_ones[:], s, k == 0, k == 7, [("n_sq", k % 2)], [("pb", b)])
            self.act(rt, self.pb(b), AF.Sqrt, [("pb", b)], ["n_rt"], bias=EPS, scale=1.0 / D)
            self.recip(rstd, rt, ["n_rt"], ["n_rstd"])
            for k in range(8):
                if final_out is None:
                    self.stt(self.hT[:, k, 2 + tt * TT: 2 + (tt + 1) * TT], self.xT[:, k, cs], gsb[:, k:k + 1], rstd,
                             ALU.mult, ALU.mult, [("xT", k, tt), "n_rstd"], [("hT", k, tt)])
                else:
                    sg = stage[k % 2]
                    self.stt(sg, self.xT[:, k, cs], gsb[:, k:k + 1], rstd, ALU.mult, ALU.mult,
                             [("xT", k, tt), "n_rstd"], [("n_stage", k % 2)])
                    self.uid += 1
                    key = "out%d" % (self.uid % 4)
                    self.dma("sync", final_out[k * 128:(k + 1) * 128, cs], sg, [("n_stage", k % 2)], [], key)
                    if key not in self.fin_keys:
                        self.fin_keys.append(key)

    def proj_fm(self, wt, wtok, tt, ncol_lo=0, ncol=128):
        b = self.bank()
        for k in range(8):
            self.mm(self.ps[0:ncol, b * 512:(b + 1) * 512], wt[:, k, ncol_lo:ncol_lo + ncol],
                    self.hT[:, k, 2 + tt * TT: 2 + (tt + 1) * TT], k == 0, k == 7,
                    [wtok, ("hT", k, tt)], [("pb", b)])
        return b

    def out_proj(self, l, mix_tiles):
        wl = self.dram["w_out"][l]
        loaded = []
        for (rt, ap, tok) in mix_tiles:
            wt, wtok = self.load_rows(wl, rt * 128, D)
            loaded.append((wt, wtok, ap, tok))
        for jo in range(8):
            for tt in range(NTT):
                cs = slice(tt * TT, (tt + 1) * TT)
                b = self.bank()
                for i, (wt, wtok, ap, tok) in enumerate(loaded):
                    self.mm(self.pb(b), wt[:, jo * 128:(jo + 1) * 128], ap[:, cs], i == 0, i == len(loaded) - 1,
                            [wtok, (tok, tt)], [("pb", b)])
                self.tt(self.xT[:, jo, cs], self.xT[:, jo, cs], self.pb(b), ALU.add, [("pb", b), ("xT", jo, tt)], [("xT", jo, tt)])

    def s5_prep(self, l):
        A = self.av
        pr = self.prm
        o = [0]
        def T(words, dt=F32):
            v = A(o[0], words, dt)
            o[0] += words
            return v
        lre, lim = pr["p_lre"][:, l, :], pr["p_lim"][:, l, :]
        dt = T(8); lr = T(8); th = T(8)
        self.act(dt, pr["p_lstep"][:, l, :], AF.Exp, [], ["q_dt"])
        self.tt(lr, lre, dt, ALU.mult, ["q_dt"], ["q_lr"])
        self.tt(th, lim, dt, ALU.mult, ["q_dt"], ["q_th"])
        NK = 24
        def b3(ap8):
            return ap8.unsqueeze(2).to_broadcast([128, 8, NK])
        kv3 = self.c_kvec[:].unsqueeze(1).to_broadcast([128, 8, NK])
        def t3():
            return T(8 * NK).rearrange("p (a b) -> p a b", b=NK)
        marg = t3(); mag = t3(); arg = t3(); a1 = t3(); a2 = t3(); argr = t3(); sn = t3(); ab = t3(); cs_ = t3()
        ai = A(o[0], 8 * NK, I32).rearrange("p (a b) -> p a b", b=NK); o[0] += 8 * NK
        self.tt(marg, b3(lr), kv3, ALU.mult, ["q_lr"], ["q_marg"])
        self.act(mag, marg, AF.Exp, ["q_marg"], ["q_mag"])
        self.tt(arg, b3(th), kv3, ALU.mult, ["q_th"], ["q_arg"])
        self.ts(a1, arg, 1.0 / TWO_PI, None, ALU.mult, None, ["q_arg"], ["q_a1"])
        self.cp(ai, a1, ["q_a1"], ["q_ai"])
        self.cp(a2, ai, ["q_ai"], ["q_a2"])
        self.stt(argr, a2, -TWO_PI, arg, ALU.mult, ALU.add, ["q_a2", "q_arg"], ["q_argr"])
        self.act(sn, argr, AF.Sin, ["q_argr"], ["q_sn"])
        self.act(ab, argr, AF.Abs, ["q_argr"], ["q_ab"])
        self.act(cs_, ab, AF.Sin, ["q_ab"], ["q_cs"], bias=math.pi / 2, scale=-1.0)
        pre = t3(); pim = t3()
        self.tt(pre, mag, cs_, ALU.mult, ["q_mag", "q_cs"], ["q_pre"])
        self.tt(pim, mag, sn, ALU.mult, ["q_mag", "q_sn"], ["q_pim"])
        nr = T(8); den = T(8); t1 = T(8); t2 = T(8); cr = T(8); ci = T(8); rden = T(8)
        self.ts(nr, pre[:, :, 6], -1.0, None, ALU.add, None, ["q_pre"], ["q_nr"])
        ni = pim[:, :, 6]
        self.tt(t1, lre, lre, ALU.mult, [], ["q_t1"])
        self.tt(t2, lim, lim, ALU.mult, [], ["q_t2"])
        self.tt(den, t1, t2, ALU.add, ["q_t1", "q_t2"], ["q_den"])
        self.recip(rden, den, ["q_den"], ["q_rden"])
        self.tt(t1, nr, lre, ALU.mult, ["q_nr", "q_den"], ["q_t1"])
        self.tt(t2, ni, lim, ALU.mult, ["q_pim", "q_den"], ["q_t2"])
        self.tt(cr, t1, t2, ALU.add, ["q_t1", "q_t2"], ["q_cr"])
        self.tt(cr, cr, rden, ALU.mult, ["q_cr", "q_rden"], ["q_cr"])
        self.tt(t1, ni, lre, ALU.mult, ["q_pim", "q_cr"], ["q_t1"])
        self.tt(t2, nr, lim, ALU.mult, ["q_nr", "q_cr"], ["q_t2"])
        self.tt(ci, t1, t2, ALU.subtract, ["q_t1", "q_t2"], ["q_ci"])
        self.tt(ci, ci, rden, ALU.mult, ["q_ci", "q_rden"], ["q_ci"])
        def t16():
            return T(128).rearrange("p (a b) -> p a b", b=16)
        def b16(ap8):
            return ap8.unsqueeze(2).to_broadcast([128, 8, 16])
        bre, bim = pr["p_bre"][:, l], pr["p_bim"][:, l]
        u1 = t16(); u2 = t16(); bbr = t16(); bbi = t16()
        self.tt(u1, b16(cr), bre, ALU.mult, ["q_cr"], ["q_u1"])
        self.tt(u2, b16(ci), bim, ALU.mult, ["q_ci"], ["q_u2"])
        self.tt(bbr, u1, u2, ALU.subtract, ["q_u1", "q_u2"], ["q_bbr"])
        self.tt(u1, b16(cr), bim, ALU.mult, ["q_cr", "q_bbr"], ["q_u1"])
        self.tt(u2, b16(ci), bre, ALU.mult, ["q_ci", "q_bbr"], ["q_u2"])
        self.tt(bbi, u1, u2, ALU.add, ["q_u1", "q_u2"], ["q_bbi"])
        def t4():
            return T(1024).rearrange("p (a b c) -> p a b c", b=8, c=16)
        w1 = t4(); w2 = t4()
        def cprod(pw_lo, vr, vi, rv, name, neg_im=False):
            pwr = pre[:, :, pw_lo:pw_lo + 8].unsqueeze(3).to_broadcast([128, 8, 8, 16])
            pwi = pim[:, :, pw_lo:pw_lo + 8].unsqueeze(3).to_broadcast([128, 8, 8, 16])
            vr4 = vr.unsqueeze(2).to_broadcast([128, 8, 8, 16])
            vi4 = vi.unsqueeze(2).to_broadcast([128, 8, 8, 16])
            xr = t4(); xi = t4()
            self.tt(w1, pwr, vr4, ALU.mult, ["q_pre"] + rv, ["q_w1"])
            self.tt(w2, pwi, vi4, ALU.mult, ["q_pim"] + rv, ["q_w2"])
            self.tt(xr, w1, w2, ALU.subtract, ["q_w1", "q_w2"], [name + "r"])
            self.tt(w1, pwr, vi4, ALU.mult, ["q_pre"] + rv, ["q_w1"])
            self.tt(w2, pwi, vr4, ALU.mult, ["q_pim"] + rv, ["q_w2"])
            if neg_im:
                self.stt(xi, w1, -1.0, w2, ALU.mult, ALU.subtract, ["q_w1", "q_w2"], [name + "i"])
            else:
                self.tt(xi, w1, w2, ALU.add, ["q_w1", "q_w2"], [name + "i"])
            return xr, xi
        gpr, gpi = cprod(0, bbr, bbi, ["q_bbr", "q_bbi"], "q_gp")
        gdr, gdi = cprod(8, bbr, bbi, ["q_bbr", "q_bbi"], "q_gd")
        cre, cim = pr["p_cre"][:, l], pr["p_cim"][:, l]
        mcr, nmci = cprod(16, cre, cim, [], "q_mc", neg_im=True)
        f2 = lambda x: x.rearrange("p a b c -> p a (b c)")
        self.cp(self.s5M[:, 0], f2(mcr), ["q_mcr"], [("s5M", l)])
        self.cp(self.s5M[:, 1], f2(nmci), ["q_mci"], [("s5M", l)])
        for ri, g in enumerate((gpr, gpi)):
            for j in range(8):
                b = self.bank()
                self.tr(self.pb(b, 128), f2(g)[:, j, :], self.c_identf[:], ["q_gpr", "q_gpi"], [("pb", b)])
                self.cp(self.s5W2[:, ri, j, :], self.pb(b, 128), [("pb", b)], [("s5W2", l)])
        for j in range(8):
            for e in range(2):
                b = self.bank()
                rows = slice(e * 64, (e + 1) * 64)
                self.mm(self.pb(b, 128), f2(gdr)[rows, j, :], f2(mcr)[rows, j, :], True, False, ["q_gdr", "q_mcr"], [("pb", b)])
                self.mm(self.pb(b, 128), f2(gdi)[rows, j, :], f2(nmci)[rows, j, :], False, True, ["q_gdi", "q_mci"], [("pb", b)])
                self.tt(self.s5W[:, 2 * j + e, :], self.pb(b, 128), self.c_wmask[:], ALU.mult, [("pb", b)], [("s5W", l)])
        self.cp(self.s5rp[:, 0, :], mag[:, :, 23], ["q_mag"], [("s5rp", l)])
        self.ts(self.s5rp[:, 1, :], th, 8.0, None, ALU.mult, None, ["q_th"], [("s5rp", l)])
        self.dma("sync", self.scr_W[l], self.s5W[:].rearrange("p a b -> p (a b)"), [("s5W", l)], [("scrW", l)], "scrst0")
        self.dma("sync", self.scr_W2[l], self.s5W2[:].rearrange("p a b c -> p (a b c)"), [("s5W2", l)], [("scrW2", l)], "scrst1")
        self.dma("sync", self.scr_M[l], self.s5M[:].rearrange("p a b c -> p (a b c)"), [("s5M", l)], [("scrM", l)], "scrst2")
        self.dma("sync", self.scr_rp[l], self.s5rp[:].rearrange("p a b -> p (a b)"), [("s5rp", l)], [("scrrp", l)], "scrst3")

    def s5_load(self, l):
        self.dma("sync", self.s5W[:].rearrange("p a b -> p (a b)"), self.scr_W[l], [("scrW", l)], ["s5Wc"], "scrld0")
        self.dma("sync", self.s5W2[:].rearrange("p a b c -> p (a b c)"), self.scr_W2[l], [("scrW2", l)], ["s5W2c"], "scrld1")
        self.dma("sync", self.s5M[:].rearrange("p a b c -> p (a b c)"), self.scr_M[l], [("scrM", l)], ["s5Mc"], "scrld2")
        self.dma("sync", self.s5rp[:].rearrange("p a b -> p (a b)"), self.scr_rp[l], [("scrrp", l)], ["s5rpc"], "scrld3")

    def phase_s5(self, l):
        A = self.av
        o = [0]
        def T(words, dt=F32):
            v = A(o[0], words, dt)
            o[0] += words
            return v
        self.s5_load(l)
        uT = T(2048, BF16).rearrange("p (a b) -> p a b", b=SEQ)
        zT = T(2048, BF16).rearrange("p (a b) -> p a b", b=SEQ)
        s5o = T(2048, BF16).rearrange("p (a b) -> p a b", b=SEQ)
        Y8 = T(2048, BF16).rearrange("p (h g c) -> p h g c", h=2, g=8)
        wl = self.dram["w_in"][l]
        for j in range(2):
            wt, wtok = self.load_cols(wl, C_U + j * 128, 128)
            for tt in range(NTT):
                b = self.proj_fm(wt, wtok, tt)
                self.act(uT[:, j, tt * TT:(tt + 1) * TT], self.pb(b), AF.Copy, [("pb", b)], [("uT", j)])
        self.tap("uT", uT, [128, 2, SEQ], [("uT", 0), ("uT", 1)])
        NB = 2
        o_tail = o[0]
        U8 = [[T(128, BF16) for e in range(2)] for _ in range(NB)]
        Sx = [T(512).rearrange("p (a b) -> p a b", b=256) for _ in range(NB)]
        tab = [T(512).rearrange("p (a b) -> p a b", b=256) for _ in range(NB)]
        wk = [T(256) for _ in range(6)]
        wki = A(o[0], 256, I32); o[0] += 256
        Sp = [T(512).rearrange("p (a b) -> p a b", b=256) for _ in range(NB)]
        Wv = [T(512).rearrange("p (a b) -> p a b", b=256) for _ in range(NB)]
        Hb = [T(256, BF16).rearrange("p (a b) -> p a b", b=256) for _ in range(NB)]
        for i in range(NB):
            self.memset(Hb[i][:, :, 0:1], 0.0, [], [("Hb", i)])
        rp = self.s5rp
        for jp in range(8):
            pb_ = jp % NB
            half = jp // 4
            for e in range(2):
                gh = (2 * jp + e) % 8
                b = self.bank()
                for s in range(8):
                    self.mm(self.pb(b, 256), self.c_zband[:, gh, 112 - 16 * s: 240 - 16 * s], uT[:, half, s::8],
                            s == 0, s == 7, [("uT", half)], [("pb", b)])
                self.act(U8[pb_][e], self.pb(b, 256), AF.Copy, [("pb", b)], [("U8", pb_, e)])
            b = self.bank()
            for ri in range(2):
                for e in range(2):
                    self.mm(self.ps[e * 64:(e + 1) * 64, b * 512 + ri * 256: b * 512 + (ri + 1) * 256],
                            self.s5W2[:, ri, jp, e * 64:(e + 1) * 64], U8[pb_][e], True, True,
                            ["s5W2c", ("U8", pb_, e)], [("pb", b)])
            self.act(Sx[pb_].rearrange("p a b -> p (a b)"), self.pb(b), AF.Copy, [("pb", b)], [("Sx", pb_)])
            phi = rp[:, 1, jp:jp + 1]
            self.ts(wk[0], self.c_mvec[:], phi, None, ALU.mult, None, ["s5rpc"], ["wk0"])
            self.ts(wk[1], wk[0], 1.0 / TWO_PI, None, ALU.mult, None, ["wk0"], ["wk1"])
            self.cp(wki, wk[1], ["wk1"], ["wki"])
            self.cp(wk[2], wki, ["wki"], ["wk2"])
            self.stt(wk[3], wk[2], -TWO_PI, wk[0], ALU.mult, ALU.add, ["wk2", "wk0"], ["wk3"])
            self.act(tab[pb_][:, 1, :], wk[3], AF.Sin, ["wk3"], [("tab", pb_)])
            self.act(wk[4], wk[3], AF.Abs, ["wk3"], ["wk4"])
            self.act(tab[pb_][:, 0, :], wk[4], AF.Sin, ["wk4"], [("tab", pb_)], bias=math.pi / 2, scale=-1.0)
            co, si = tab[pb_][:, 0, :], tab[pb_][:, 1, :]
            sre, sim = Sx[pb_][:, 0, :], Sx[pb_][:, 1, :]
            self.tt(wk[0], sre, co, ALU.mult, [("Sx", pb_), ("tab", pb_)], ["wk0"])
            self.tt(wk[1], sim, si, ALU.mult, [("Sx", pb_), ("tab", pb_)], ["wk1"])
            self.tt(Sp[pb_][:, 0, :], wk[0], wk[1], ALU.add, ["wk0", "wk1"], [("Sp", pb_)])
            self.tt(wk[2], sim, co, ALU.mult, [("Sx", pb_), ("tab", pb_)], ["wk2"])
            self.tt(wk[3], sre, si, ALU.mult, [("Sx", pb_), ("tab", pb_)], ["wk3"])
            self.tt(Sp[pb_][:, 1, :], wk[2], wk[3], ALU.subtract, ["wk2", "wk3"], [("Sp", pb_)])
            self.cp(wk[5], rp[:, 0, jp:jp + 1].to_broadcast([128, 256]), ["s5rpc"], ["wk5"])
            for ri in range(2):
                self.scan(Wv[pb_][:, ri, :], wk[5], Sp[pb_][:, ri, :], ["wk5", ("Sp", pb_)], [("Wv", pb_)])
            wre, wim = Wv[pb_][:, 0, 0:255], Wv[pb_][:, 1, 0:255]
            c5, s5_ = co[:, 0:255], si[:, 0:255]
            self.tt(wk[0][:, 0:255], wre, c5, ALU.mult, [("Wv", pb_), ("tab", pb_)], ["wk0"])
            self.tt(wk[1][:, 0:255], wim, s5_, ALU.mult, [("Wv", pb_), ("tab", pb_)], ["wk1"])
            self.tt(Hb[pb_][:, 0, 1:256], wk[0][:, 0:255], wk[1][:, 0:255], ALU.subtract, ["wk0", "wk1"], [("Hb", pb_)])
            self.tt(wk[2][:, 0:255], wim, c5, ALU.mult, [("Wv", pb_), ("tab", pb_)], ["wk2"])
            self.tt(wk[3][:, 0:255], wre, s5_, ALU.mult, [("Wv", pb_), ("tab", pb_)], ["wk3"])
            self.tt(Hb[pb_][:, 1, 1:256], wk[2][:, 0:255], wk[3][:, 0:255], ALU.add, ["wk2", "wk3"], [("Hb", pb_)])
            for e in range(2):
                gh = (2 * jp + e) % 8
                rows = slice(e * 64, (e + 1) * 64)
                b = self.bank()
                self.mm(self.pb(b, 256), self.s5W[:, 2 * jp + e, :], U8[pb_][e], True, False, ["s5Wc", ("U8", pb_, e)], [("pb", b)])
                self.mm(self.pb(b, 256), self.s5M[rows, 0, jp, :], Hb[pb_][rows, 0, :], False, False, ["s5Mc", ("Hb", pb_)], [("pb", b)])
                self.mm(self.pb(b, 256), self.s5M[rows, 1, jp, :], Hb[pb_][rows, 1, :], False, True, ["s5Mc", ("Hb", pb_)], [("pb", b)])
                self.act(Y8[:, half, gh, :], self.pb(b, 256), AF.Copy, [("pb", b)], [("Y8", half)])
        self.barrier()
        o[0] = o_tail
        ytmp = [T(256), T(256)]
        dsk = self.prm["p_s5d"]
        for half in range(2):
            for t in range(8):
                b = self.bank()
                for gh in range(8):
                    self.mm(self.pb(b, 256), self.c_zband[:, t, 112 - 16 * gh: 240 - 16 * gh], Y8[:, half, gh, :],
                            gh == 0, gh == 7, [("Y8", half)], [("pb", b)])
                yt = ytmp[t % 2]
                self.stt(yt, uT[:, half, t::8], dsk[:, l, half:half + 1], self.pb(b, 256), ALU.mult, ALU.add,
                         [("uT", half), ("pb", b)], [("ytmp", t % 2)])
                self.act(zT[:, half, t::8], yt, AF.Gelu_apprx_tanh, [("ytmp", t % 2)], [("zT", half)])
        wg = self.dram["s5_w_glu"][l]
        g0, g0t = self.load_rows(wg, 0, 256)
        g1, g1t = self.load_rows(wg, 128, 256)
        sg = [T(512), T(512)]
        ii = 0
        for jo in range(2):
            for tt in range(NTT):
                cs = slice(tt * TT, (tt + 1) * TT)
                b = self.bank()
                self.mm(self.pb(b), g0[:, jo * 128:(jo + 1) * 128], zT[:, 0, cs], True, False, [g0t, ("zT", 0)], [("pb", b)])
                self.mm(self.pb(b), g1[:, jo * 128:(jo + 1) * 128], zT[:, 1, cs], False, True, [g1t, ("zT", 1)], [("pb", b)])
                s_ = sg[ii % 2]
                self.act(s_, self.pb(b), AF.Sigmoid, [("pb", b)], [("sg", ii % 2)])
                self.tt(s5o[:, jo, cs], zT[:, jo, cs], s_, ALU.mult, [("zT", jo), ("sg", ii % 2)], [(("s5o", jo), tt)])
                ii += 1
        self.tap("s5o", s5o, [128, 2, SEQ], [(("s5o", j), t) for j in range(2) for t in range(NTT)])
        self.out_proj(l, [(0, s5o[:, 0, :], ("s5o", 0)), (1, s5o[:, 1, :], ("s5o", 1))])


    def phase_att(self, l):
        A = self.av
        o = [0]
        def T(words, dt=F32):
            v = A(o[0], words, dt)
            o[0] += words
            return v
        self.rr = [0, 1, 2, 3, 4]
        OB = [5, 6]
        atto = T(3072, BF16).rearrange("p (a b) -> p a b", b=SEQ)
        q2 = [T(1024, BF16) for _ in range(2)]
        k2 = [T(1024, BF16) for _ in range(2)]
        Vaug = T(2048, BF16).rearrange("p (t h c) -> p t h c", t=16, h=2)
        PT = [T(256, BF16) for _ in range(3)]
        dtmp = [T(128) for _ in range(2)]
        rs = [T(512) for _ in range(2)]
        attp = T(2048)
        sq = [T(256, BF16) for _ in range(2)]
        rt = T(512)
        rstd = T(512)
        wl = self.dram["w_in"][l]
        lam_init = 0.8 - 0.6 * math.exp(-0.3 * l)
        self.memset(Vaug[:, :, 0, 64:128], 1.0, [], ["Vones"])
        self.memset(Vaug[:, :, 1, 0:64], 1.0, [], ["Vones"])
        ipt = 0
        idt = 0
        irs = 0
        iob = 0
        for hp in range(3):
            for i in range(2):
                h = 2 * hp + i
                for (dst, c0, tab, nm) in ((q2[i], C_Q, self.dram["c_qtab"], "q2"), (k2[i], C_K, self.dram["c_ktab"], "k2")):
                    wt, wtok = self.load_cols(wl, c0 + h * 64, 64)
                    for tt in range(NTT):
                        cs = slice(tt * TT, (tt + 1) * TT)
                        b = self.proj_fm(wt, wtok, tt, 0, 64)
                        self.act(dst[0:32, cs], self.ps[0:32, b * 512:(b + 1) * 512], AF.Copy, [("pb", b)], [(nm, i, 0)])
                        self.act(dst[64:96, cs], self.ps[32:64, b * 512:(b + 1) * 512], AF.Copy, [("pb", b)], [(nm, i, 1)])
                    for c in range(2):
                        self.uid += 1
                        self.dma("gpsimd", dst[c * 64 + 32: c * 64 + 38, :], tab[h], [], [(nm, i, c)], "bias%d" % (self.uid % 8))
            wt, wtok = self.load_cols(wl, C_V + hp * 128, 128)
            for i4 in range(4):
                b = self.bank()
                for a in range(4):
                    i16 = 4 * i4 + a
                    for k in range(8):
                        self.mm(self.pb(b, 128, a * 128), self.hT[:, k, 2 + i16 * 128: 2 + (i16 + 1) * 128], wt[:, k, :],
                                k == 0, k == 7, [wtok, ("hT", k, i4)], [("pb", b)])
                src = self.pb(b).rearrange("p (a h v) -> p a h v", a=4, h=2)
                self.act(Vaug[:, 4 * i4:4 * i4 + 4, 0, 0:64], src[:, :, 0, :], AF.Copy, [("pb", b), "Vones"], [("V", i4)])
                self.cp(Vaug[:, 4 * i4:4 * i4 + 4, 1, 64:128], src[:, :, 1, :], [("pb", b), "Vones"], [("V", i4)])
            for hh in range(2):
                h = 2 * hp + hh
                slope = SLOPES[h]
                vrows = slice(hh * 64, (hh + 1) * 64)
                srows = slice((1 - hh) * 64, (2 - hh) * 64)
                for c in range(2):
                    kr = slice(c * 64, c * 64 + 38)
                    for qg in range(4):
                        ob = OB[iob % 2]
                        iob += 1
                        nkb = 4 * qg + 4
                        for kb in range(nkb):
                            j = kb - 4 * qg
                            q0 = qg * 512 if j < 0 else kb * 128
                            N = 512 if j < 0 else (4 - j) * 128
                            ooff = 0 if j < 0 else j * 128
                            bimm = float(slope * (kb * 128 - qg * 512))
                            b = self.bank()
                            self.mm(self.pb(b, N), k2[hh][kr, kb * 128:(kb + 1) * 128], q2[hh][kr, q0:q0 + N], True, True,
                                    [("k2", hh, c), ("q2", hh, c)], [("pb", b)])
                            pt = PT[ipt % 3]
                            ptk = ("PT", ipt % 3)
                            ipt += 1
                            if j < 0:
                                self.act(pt[:, 0:N], self.pb(b, N), AF.Exp, [("pb", b)], [ptk], bias=bimm, scale=SCALE)
                            else:
                                dt_ = dtmp[idt % 2]
                                dtk = ("dtmp", idt % 2)
                                idt += 1
                                self.tt(dt_, self.pb(b, 128), self.c_corr[:, h, :], ALU.add, [("pb", b)], [dtk])
                                self.act(pt[:, 0:128], dt_, AF.Exp, [dtk], [ptk], bias=bimm, scale=SCALE)
                                if N > 128:
                                    self.act(pt[:, 128:N], self.pb(b, N - 128, 128), AF.Exp, [("pb", b)], [ptk], bias=bimm, scale=SCALE)
                            self.mm(self.pb(ob, N, ooff), Vaug[:, kb, hh, :], pt[:, 0:N], kb == 0, kb == nkb - 1,
                                    [("V", kb // 4), ptk], [("pb", ob)])
                        r_ = rs[irs % 2]
                        rk = ("rs", irs % 2)
                        irs += 1
                        cs = slice(qg * 512, (qg + 1) * 512)
                        self.recip(r_[srows, :], self.ps[srows, ob * 512:(ob + 1) * 512], [("pb", ob)], [rk])
                        if c == 0:
                            self.tt(attp[vrows, cs], self.ps[vrows, ob * 512:(ob + 1) * 512], r_[srows, :], ALU.mult,
                                    [("pb", ob), rk], [("attp", hh, qg)])
                        else:
                            self.tt(r_[vrows, :], self.ps[vrows, ob * 512:(ob + 1) * 512], r_[srows, :], ALU.mult,
                                    [("pb", ob), rk], [rk])
                            self.stt(attp[vrows, cs], r_[vrows, :], self.lamv[vrows, l:l + 1], attp[vrows, cs], ALU.mult, ALU.add,
                                     [rk, ("attp", hh, qg)], [("attp", hh, qg)])
            sc_ = 1.0 / (64.0 * (1.0 - lam_init) ** 2)
            bs_ = EPS / (1.0 - lam_init) ** 2
            for tt in range(NTT):
                cs = slice(tt * TT, (tt + 1) * TT)
                s_ = sq[tt % 2]
                self.act(s_, attp[:, cs], AF.Square, [("attp", 0, tt), ("attp", 1, tt)], [("asq", tt % 2)])
                b = self.bank()
                self.mm(self.pb(b), self.c_bones[:], s_, True, True, [("asq", tt % 2)], [("pb", b)])
                self.act(rt, self.pb(b), AF.Sqrt, [("pb", b)], ["a_rt"], bias=bs_, scale=sc_)
                self.recip(rstd, rt, ["a_rt"], ["a_rstd"])
                self.stt(atto[:, hp, cs], attp[:, cs], self.prm["p_subg"][:, l:l + 1], rstd, ALU.mult, ALU.mult,
                         [("attp", 0, tt), ("attp", 1, tt), "a_rstd"], [(("atto", hp), tt)])
        self.tap("atto", atto, [128, 3, SEQ], [(("atto", j), t) for j in range(3) for t in range(NTT)])
        self.rr = list(range(7))
        self.out_proj(l, [(2 + hp, atto[:, hp, :], ("atto", hp)) for hp in range(3)])


    def phase_hgrn(self, l):
        A = self.av
        o = [0]
        def T(words, dt=F32):
            v = A(o[0], words, dt)
            o[0] += words
            return v
        self.rr = [0, 1, 2]
        OB = [5, 6]
        SCB = [3, 4]
        ig = 0
        hgo = T(3072, BF16).rearrange("p (a b) -> p a b", b=SEQ)
        QsT = T(1024, BF16)
        KpT = T(1024, BF16)
        Ketok = T(1024, BF16).rearrange("p (t c) -> p t c", c=128)
        Vtok = T(1024, BF16).rearrange("p (t c) -> p t c", c=128)
        sgT = T(1024, BF16)
        stb = T(1024, BF16).rearrange("p (c v) -> p c v", v=64)
        dec = T(32)
        t0 = T(512); t2 = T(512); t3 = T(512); t4 = T(512); t5 = T(512)
        kkb = T(256, BF16)
        KeT = [T(256, BF16) for _ in range(2)]
        AT = [T(256, BF16) for _ in range(2)]
        stf = [T(64) for _ in range(2)]
        osq = [T(256, BF16) for _ in range(2)]
        rt, rstd, ot = t0, t2, t3
        wl = self.dram["w_in"][l]
        for hp in range(3):
            wcf, wcft = self.load_cols(wl, C_CF + hp * 128, 128)
            wcq, wcqt = self.load_cols(wl, C_CQ + hp * 128, 128)
            wcg, wcgt = self.load_cols(wl, C_CG + hp * 128, 128)
            wci, wcit = self.load_cols(wl, C_CI + hp * 128, 128)
            lbc = self.lb[:, l, hp:hp + 1]
            omc = self.oml[:, l, hp:hp + 1]
            for tt in range(NTT):
                cs = slice(tt * TT, (tt + 1) * TT)
                b = self.proj_fm(wcf, wcft, tt)
                self.act(t0, self.pb(b), AF.Sigmoid, [("pb", b)], ["h_t0"])
                self.ts(t2, t0, omc, lbc, ALU.mult, ALU.add, ["h_t0"], ["h_t2"])
                self.ts(kkb, t2, -1.0, 1.0, ALU.mult, ALU.add, ["h_t2"], ["h_kk"])
                self.ts(t3, t2, 1e-6, None, ALU.max, None, ["h_t2"], ["h_t3"])
                self.act(t0, t3, AF.Ln, ["h_t3"], ["h_t0"])
                self.scan(t4, self.c_scanmask[:], t0, ["h_t0"], ["h_bc"])
                self.act(t5, t4, AF.Exp, ["h_bc"], ["h_t5"])
                b = self.proj_fm(wcq, wcqt, tt)
                self.act(t3, self.pb(b), AF.Silu, [("pb", b)], ["h_t3"])
                self.tt(QsT[:, cs], t3, t5, ALU.mult, ["h_t3", "h_t5"], [("hq", tt)])
                self.act(t5, t4, AF.Exp, ["h_bc"], ["h_t5"], scale=-1.0)
                self.tt(KpT[:, cs], kkb, t5, ALU.mult, ["h_kk", "h_t5"], [("hk", tt)])
                bc3 = t4.rearrange("p (c t) -> p c t", t=64)
                self.tt(t2.rearrange("p (c t) -> p c t", t=64), bc3[:, :, 63:64].to_broadcast([128, 8, 64]), bc3,
                        ALU.subtract, ["h_bc"], ["h_t2"])
                self.act(t2, t2, AF.Exp, ["h_t2"], ["h_t2"])
                ke = KeT[tt % 2]
                self.tt(ke, kkb, t2, ALU.mult, ["h_kk", "h_t2"], [("KeT", tt % 2)])
                self.act(dec[:, tt * 8:(tt + 1) * 8], bc3[:, :, 63], AF.Exp, ["h_bc"], [("hdec", tt)])
                half = tt % 2
                for a in range(4):
                    self.tr(self.psb[:, half * 512 + a * 128: half * 512 + (a + 1) * 128], ke[:, a * 128:(a + 1) * 128],
                            self.c_ident[:], [("KeT", tt % 2)], [("psb", half)])
                self.cp(Ketok[:, 4 * tt:4 * tt + 4, :].rearrange("p a c -> p (a c)"), self.psb[:, half * 512:(half + 1) * 512],
                        [("psb", half)], [("hke", tt)])
                b = self.proj_fm(wcg, wcgt, tt)
                self.act(sgT[:, cs], self.pb(b), AF.Silu, [("pb", b)], [("hsg", tt)])
            for i4 in range(4):
                b = self.bank()
                for a in range(4):
                    i16 = 4 * i4 + a
                    for k in range(8):
                        self.mm(self.pb(b, 128, a * 128), self.hT[:, k, 2 + i16 * 128: 2 + (i16 + 1) * 128], wci[:, k, :],
                                k == 0, k == 7, [wcit, ("hT", k, i4)], [("pb", b)])
                self.act(Vtok[:, 4 * i4:4 * i4 + 4, :].rearrange("p a c -> p (a c)"), self.pb(b), AF.Copy, [("pb", b)], [("hv", i4)])
            if getattr(self, "hg_stop", 9) <= 1:
                self.tap("hq", QsT, [128, SEQ], [("hq", t) for t in range(NTT)])
                self.tap("hk", KpT, [128, SEQ], [("hk", t) for t in range(NTT)])
                self.tap("hke", Ketok, [128, 16, 128], [("hke", t) for t in range(NTT)])
                self.tap("hv", Vtok, [128, 16, 128], [("hv", t) for t in range(NTT)])
                return
            self.memset(stb[:, 0, :], 0.0, [], [("hst", 0)])
            for g in range(4):
                dsb = self.bank()
                for par in range(2):
                    prow = slice(par * 64, par * 64 + 64)
                    for cc in range(4):
                        c = 8 * g + 2 * cc + par
                        i16 = c // 2
                        off = dsb * 512 + (c % 8) * 64
                        for hh in range(2):
                            hc = slice(hh * 64, (hh + 1) * 64)
                            self.mm(self.ps[hc, off:off + 64], Ketok[prow, i16, hc], Vtok[prow, i16, hc], True, True,
                                    [("hke", g), ("hv", g)], [("pb", dsb)])
                for c in range(8 * g, 8 * g + 8):
                    if c == 31:
                        break
                    off = dsb * 512 + (c % 8) * 64
                    nxt = stf[(c + 1) % 2]
                    if c == 0:
                        self.cp(nxt, self.ps[:, off:off + 64], [("pb", dsb)], [("stf", (c + 1) % 2)])
                    else:
                        self.stt(nxt, stf[c % 2], dec[:, c:c + 1], self.ps[:, off:off + 64], ALU.mult, ALU.add,
                                 [("stf", c % 2), ("hdec", c // 8), ("pb", dsb)], [("stf", (c + 1) % 2)])
                    self.act(stb[:, c + 1, :], nxt, AF.Copy, [("stf", (c + 1) % 2)], [("hst", c + 1)])
            for g in range(4):
                tt = g
                scb = SCB[ig % 2]
                ob = OB[ig % 2]
                at = AT[ig % 2]
                atk = ("AT", ig % 2)
                ig += 1
                for hh in range(2):
                    hc = slice(hh * 64, (hh + 1) * 64)
                    for c in range(8 * g, 8 * g + 8):
                        par, cc = c % 2, (c % 8) // 2
                        prow = slice(par * 64, par * 64 + 64)
                        ccs = slice(c * 64, (c + 1) * 64)
                        so = scb * 512 + hh * 256 + cc * 64
                        self.mm(self.ps[prow, so:so + 64], KpT[hc, ccs], QsT[hc, ccs], True, True,
                                [("hk", tt), ("hq", tt)], [("pb", scb)])
                self.tt(at.rearrange("p (a t) -> p a t", t=64), self.pb(scb).rearrange("p (a t) -> p a t", t=64),
                        self.c_hmask[:].unsqueeze(1).to_broadcast([128, 8, 64]), ALU.mult, [("pb", scb)], [atk])
                first = {0: True, 1: True}
                for par in range(2):
                    prow = slice(par * 64, par * 64 + 64)
                    for cc in range(4):
                        c = 8 * g + 2 * cc + par
                        i16 = c // 2
                        oo = ob * 512 + (c % 8) * 64
                        for hh in range(2):
                            hc = slice(hh * 64, (hh + 1) * 64)
                            ao = hh * 256 + cc * 64
                            self.mm(self.ps[hc, oo:oo + 64], Vtok[prow, i16, hc], at[prow, ao:ao + 64], first[hh], False,
                                    [("hv", g), atk], [("pb", ob)], sgc=True)
                            first[hh] = False
                for hh in range(2):
                    hc = slice(hh * 64, (hh + 1) * 64)
                    for c in range(8 * g, 8 * g + 8):
                        ccs = slice(c * 64, (c + 1) * 64)
                        oo = ob * 512 + (c % 8) * 64
                        self.mm(self.ps[hc, oo:oo + 64], stb[hc, c, :], QsT[hc, ccs], False, True,
                                [("hst", c), ("hq", tt)], [("pb", ob)], sgc=True)
                cs = slice(tt * TT, (tt + 1) * TT)
                s_ = osq[tt % 2]
                self.act(s_, self.pb(ob), AF.Square, [("pb", ob)], [("osq", tt % 2)])
                b = self.bank()
                self.mm(self.pb(b), self.c_bones[:], s_, True, True, [("osq", tt % 2)], [("pb", b)])
                self.act(rt, self.pb(b), AF.Sqrt, [("pb", b)], ["h_t0"], bias=EPS, scale=1.0 / 64.0)
                self.recip(rstd, rt, ["h_t0"], ["h_t2"])
                self.stt(ot, self.pb(ob), self.prm["p_hng"][:, l:l + 1], rstd, ALU.mult, ALU.mult, [("pb", ob), "h_t2"], ["h_t3"])
                self.tt(hgo[:, hp, cs], ot, sgT[:, cs], ALU.mult, ["h_t3", ("hsg", tt)], [(("hgo", hp), tt)])
        self.tap("hgo", hgo, [128, 3, SEQ], [(("hgo", j), t) for j in range(3) for t in range(NTT)])
        self.rr = list(range(7))
        self.out_proj(l, [(5 + hp, hgo[:, hp, :], ("hgo", hp)) for hp in range(3)])


    def phase_ffn(self, l):
        A = self.av
        o = [0]
        def T(words, dt=F32):
            v = A(o[0], words, dt)
            o[0] += words
            return v
        gT = T(NFF * 512, BF16).rearrange("p (j t) -> p j t", t=1024)
        abuf = [T(1024), T(1024)]
        ge = T(1024)
        self.rr = [5, 6]
        UPB, GB = [0, 1, 2], [3, 4]
        up = self.ps[:, 0:1536]
        gt = self.ps[:, 1536:2560]
        cw = self.prm["p_convw"]
        cb = self.prm["p_convb"]
        wu_l, wg_l, wd_l = self.dram["w_up"][l], self.dram["w_gate"][l], self.dram["w_down"][l]
        upt = [("pb", b) for b in UPB]
        gtt = [("pb", b) for b in GB]
        for hf in range(2):
            t0 = hf * 1024
            for j in range(NFF):
                wu, wut = self.load_cols(wu_l, j * 128, 128)
                wg, wgt = self.load_cols(wg_l, j * 128, 128)
                for (c0, n, h0) in ((0, 512, t0), (512, 512, t0 + 512), (1024, 2, t0 + 1024)):
                    for k in range(8):
                        self.mm(up[:, c0:c0 + n], wu[:, k, :], self.hT[:, k, h0:h0 + n], k == 0, k == 7, [wut], [("pb", c0 // 512)])
                for t2 in range(2):
                    for k in range(8):
                        self.mm(gt[:, t2 * 512:(t2 + 1) * 512], wg[:, k, :], self.hT[:, k, t0 + 2 + t2 * 512: t0 + 2 + (t2 + 1) * 512],
                                k == 0, k == 7, [wgt], [("pb", 3 + t2)])
                a = abuf[j % 2]
                ak = ("ffa", j % 2)
                self.act(a, up[:, 2:1026], AF.Identity, upt, [ak], bias=cb[:, l, j:j + 1], scale=cw[:, l, 2, j:j + 1])
                self.stt(a, up[:, 1:1025], cw[:, l, 1, j:j + 1], a, ALU.mult, ALU.add, upt + [ak], [ak])
                self.stt(a, up[:, 0:1024], cw[:, l, 0, j:j + 1], a, ALU.mult, ALU.add, upt + [ak], [ak])
                self.act(ge, a, AF.Gelu_apprx_tanh, [ak], ["ffge"])
                self.tt(gT[:, j, :], ge, gt, ALU.mult, ["ffge"] + gtt, [("gT", j)])
            for jo in range(8):
                wd, wdt = self.load_bigcols(wd_l, jo * 128, NFF)
                for t2 in range(2):
                    b = self.bank()
                    cs = slice(t0 + t2 * 512, t0 + (t2 + 1) * 512)
                    for f in range(NFF):
                        self.mm(self.pb(b), wd[:, f, :], gT[:, f, t2 * 512:(t2 + 1) * 512], f == 0, f == NFF - 1,
                                [wdt, ("gT", f)], [("pb", b)])
                    tt = (t0 + t2 * 512) // TT
                    self.tt(self.xT[:, jo, cs], self.xT[:, jo, cs], self.pb(b), ALU.add, [("pb", b), ("xT", jo, tt)], [("xT", jo, tt)])
        self.rr = list(range(7))

    def build(self):
        self.setup()
        for s in range(self.nseq):
            for k in range(8):
                self.dma("sync", self.xT[:, k, :], self.x_in[s, k * 128:(k + 1) * 128, :], [],
                         [("xT", k, t) for t in range(NTT)], "ldx%d" % (k % 4))
            for l in range(self.depth):
                self.rmsnorm(self.prm["p_gmix"][:, l, :])
                self.tap("hT", self.hT[:], [128, 8, SEQ + 2], [("hT", k, t) for k in range(8) for t in range(NTT)])
                self.barrier()
                if "s5" in self.phases:
                    self.phase_s5(l)
                    self.barrier()
                if "att" in self.phases:
                    self.phase_att(l)
                    self.barrier()
                if "hgrn" in self.phases:
                    self.phase_hgrn(l)
                    self.barrier()
                self.tap("x1", self.xT[:], [128, 8, SEQ], [("xT", k, t) for k in range(8) for t in range(NTT)])
                if "ffn" in self.phases:
                    self.rmsnorm(self.prm["p_gffn"][:, l, :])
                    self.barrier()
                    self.phase_ffn(l)
                    self.barrier()
                self.tap("x2", self.xT[:], [128, 8, SEQ], [("xT", k, t) for k in range(8) for t in range(NTT)])
            self.rmsnorm(self.prm["p_gfin"][:], final_out=self.y_out[s])
            self.barrier()
        self.P.emit(final_wait_keys=self.fin_keys)
        return self.nc


_CACHE = {}


def kernel(**inputs):
    ncores = 8
    x = np.asarray(inputs["x"], dtype=np.float32)
    nb = x.shape[0]
    per = nb // ncores
    if "nc" not in _CACHE:
        _CACHE["nc"] = Builder(nseq=per, depth=DEPTH).build()
    nc = _CACHE["nc"]
    common = {}
    common.update(make_consts())
    common.update(layout_params({k: np.asarray(v) for k, v in inputs.items()}))
    for k in BIGW:
        common[k] = np.ascontiguousarray(np.asarray(inputs[k]), dtype=np.float32)
    in_maps = []
    for c in range(ncores):
        m = dict(common)
        m["xT"] = np.ascontiguousarray(x[c * per:(c + 1) * per].transpose(0, 2, 1))
        in_maps.append(m)
    res = run_bass_kernel_spmd(nc, in_maps, core_ids=list(range(ncores)))
    out = np.empty((nb, SEQ, D), np.float32)
    for c in range(ncores):
        out[c * per:(c + 1) * per] = np.asarray(res.results[c]["yT"]).transpose(0, 2, 1)
    return out
```
